# Optimizing a Trainium2 kernel written in Bass

```python
import jax, jax.numpy as jnp
from jax import lax
import numpy as np

D_MODEL = 1024
BATCH = 8
SEQ = 2048
DEPTH = 1

N_META = 16
HEAD_DIM = 64
MIX_WIDTH = D_MODEL
W_A = MIX_WIDTH // 2
W_B = MIX_WIDTH - W_A
N_HEADS_A = W_A // HEAD_DIM
N_HEADS_B = W_B // HEAD_DIM
CONV_A_WIDTH = 3
CONV_B_WIDTH = 31
IN_PROJ_WIDTH = 3 * W_A + 2 * W_B
D_FF = ((8 * D_MODEL // 3 + 255) // 256) * 256
RMS_EPS = 1e-6
LN_EPS = 1e-5

kernel_name = "hybrid_shortconv_conformer_block"


def rms_norm(x, g):
    xf = x.astype(jnp.float32)
    y = xf * lax.rsqrt(jnp.mean(xf * xf, axis=-1, keepdims=True) + RMS_EPS)
    return (y * g.astype(jnp.float32)).astype(x.dtype)


def layer_norm(x, g, b):
    xf = x.astype(jnp.float32)
    mu = jnp.mean(xf, axis=-1, keepdims=True)
    var = jnp.mean(jnp.square(xf - mu), axis=-1, keepdims=True)
    y = (xf - mu) * lax.rsqrt(var + LN_EPS)
    return (y * g.astype(jnp.float32) + b.astype(jnp.float32)).astype(x.dtype)


def causal_depthwise_conv(u, w):
    k = w.shape[0]
    return lax.conv_general_dilated(
        u, w[:, None, :].astype(u.dtype),
        window_strides=(1,), padding=[(k - 1, 0)],
        dimension_numbers=("NWC", "WIO", "NWC"),
        feature_group_count=u.shape[-1])


def mixer(xn, w_in, conv_a_w, conv_b_w, conv_b_bias, ln_b_gain, ln_b_bias, w_out):
    h = jnp.einsum("btd,de->bte", xn, w_in)
    b_gate, c_gate, h_a, glu_val, glu_gate = jnp.split(
        h, [W_A, 2 * W_A, 3 * W_A, 3 * W_A + W_B], axis=-1)
    y_a = b_gate * causal_depthwise_conv(c_gate * h_a, conv_a_w)
    g = glu_val * jax.nn.sigmoid(glu_gate)
    z = causal_depthwise_conv(g, conv_b_w) + conv_b_bias.astype(g.dtype)
    y_b = jax.nn.silu(layer_norm(z, ln_b_gain, ln_b_bias))
    y = jnp.concatenate([y_a, y_b], axis=-1)
    return jnp.einsum("bte,ed->btd", y, w_out)


def swiglu(xn, w_gate, w_up, w_down):
    a = jnp.einsum("btd,df->btf", xn, w_gate)
    u = jnp.einsum("btd,df->btf", xn, w_up)
    return jnp.einsum("btf,fd->btd", jax.nn.silu(a) * u, w_down)


def setup_inputs(seed: int = 0) -> dict:
    key = jax.random.key(seed)
    ks = jax.random.split(key, 20)
    f32 = jnp.float32

    def nrm(k, shape, scale):
        return jax.random.normal(k, shape, f32) * scale

    def gain(k, n):
        return jnp.ones((DEPTH, n), f32) + 0.05 * jax.random.normal(k, (DEPTH, n), f32)

    return {
        "x": jax.random.normal(ks[0], (BATCH, SEQ, D_MODEL), f32),
        "meta_tokens": nrm(ks[1], (N_META, D_MODEL), 1.0),
        "pre_mix_norm": gain(ks[2], D_MODEL),
        "w_in": nrm(ks[3], (DEPTH, D_MODEL, IN_PROJ_WIDTH), D_MODEL ** -0.5),
        "conv_a_w": nrm(ks[4], (DEPTH, CONV_A_WIDTH, W_A), CONV_A_WIDTH ** -0.5),
        "conv_b_w": nrm(ks[5], (DEPTH, CONV_B_WIDTH, W_B), CONV_B_WIDTH ** -0.5),
        "conv_b_bias": nrm(ks[6], (DEPTH, W_B), 0.02),
        "ln_b_gain": gain(ks[7], W_B),
        "ln_b_bias": nrm(ks[8], (DEPTH, W_B), 0.02),
        "w_out": nrm(ks[9], (DEPTH, MIX_WIDTH, D_MODEL), MIX_WIDTH ** -0.5),
        "post_mix_norm": gain(ks[10], D_MODEL),
        "pre_ffn_norm": gain(ks[11], D_MODEL),
        "w_gate": nrm(ks[12], (DEPTH, D_MODEL, D_FF), D_MODEL ** -0.5),
        "w_up": nrm(ks[13], (DEPTH, D_MODEL, D_FF), D_MODEL ** -0.5),
        "w_down": nrm(ks[14], (DEPTH, D_FF, D_MODEL), D_FF ** -0.5),
        "post_ffn_norm": gain(ks[15], D_MODEL),
    }


def reference(x, meta_tokens, pre_mix_norm, w_in, conv_a_w, conv_b_w, conv_b_bias,
              ln_b_gain, ln_b_bias, w_out, post_mix_norm, pre_ffn_norm,
              w_gate, w_up, w_down, post_ffn_norm):
    b = x.shape[0]
    meta = jnp.broadcast_to(meta_tokens[None].astype(x.dtype), (b, N_META, x.shape[-1]))
    h = jnp.concatenate([meta, x], axis=1)
    for l in range(DEPTH):
        mix = mixer(rms_norm(h, pre_mix_norm[l]), w_in[l], conv_a_w[l], conv_b_w[l],
                    conv_b_bias[l], ln_b_gain[l], ln_b_bias[l], w_out[l])
        h = h + rms_norm(mix, post_mix_norm[l])
        ff = swiglu(rms_norm(h, pre_ffn_norm[l]), w_gate[l], w_up[l], w_down[l])
        h = h + rms_norm(ff, post_ffn_norm[l])
    return h[:, N_META:, :]
```

```python
import contextlib
import numpy as np
import concourse.bass as bass
import concourse.mybir as mybir
from concourse.bass_utils import run_bass_kernel_spmd

F32 = mybir.dt.float32
BF16 = mybir.dt.bfloat16
F32R = mybir.dt.float32r
AF = mybir.ActivationFunctionType
ALU = mybir.AluOpType

ENGS = ("pe", "act", "dve", "pool", "sp")


class Prog:
    def __init__(self):
        self.ops = {e: [] for e in ENGS}
        self.cnt = {e: 0 for e in ENGS}
        self.dcnt = {}
        self.lastw = {}
        self.reads = {}

    def _deps(self, r, w, extra):
        deps = list(extra)
        for k in r:
            if k in self.lastw:
                deps.append(self.lastw[k])
        for k in w:
            if k in self.lastw:
                deps.append(self.lastw[k])
            deps.extend(self.reads.get(k, ()))
        return deps

    def _commit(self, t, r, w):
        for k in r:
            self.reads.setdefault(k, []).append(t)
        for k in w:
            self.lastw[k] = t
            self.reads[k] = []

    def op(self, eng, fn, r=(), w=(), extra=()):
        deps = self._deps(r, w, extra)
        self.cnt[eng] += 1
        t = ("E", eng, self.cnt[eng])
        self.ops[eng].append(("op", fn, deps, None))
        self._commit(t, r, w)
        return t

    def group(self, eng, fns, r=(), w=(), extra=()):
        deps = self._deps(r, w, extra)
        self.ops[eng].append(("wait", None, deps, None))
        for f in fns[:-1]:
            self.ops[eng].append(("quiet", f, [], None))
        self.cnt[eng] += 1
        t = ("E", eng, self.cnt[eng])
        self.ops[eng].append(("op", fns[-1], [], None))
        self._commit(t, r, w)
        return t

    def fence(self):
        return [("E", e, c) for e, c in self.cnt.items() if c] + \
               [("D", k, c) for k, c in self.dcnt.items()]

    def quiet(self, eng, fn):
        self.ops[eng].append(("quiet", fn, [], None))

    def dma(self, eng, fn, key, r=(), w=(), extra=()):
        deps = self._deps(r, w, extra)
        self.dcnt[key] = self.dcnt.get(key, 0) + 16
        t = ("D", key, self.dcnt[key])
        self.ops[eng].append(("dma", fn, deps, key))
        self._commit(t, r, w)
        return t

    def wait(self, eng, deps):
        self.ops[eng].append(("wait", None, list(deps), None))

    def emit(self, nc, stack):
        sems = {e: stack.enter_context(nc.semaphore("s_" + e)) for e in ENGS}
        dsems = {k: stack.enter_context(nc.semaphore("d_%d" % i))
                 for i, k in enumerate(self.dcnt)}
        ops = self.ops

        def make(ename):
            def body(eng):
                seen = {}
                for kind, fn, deps, key in ops[ename]:
                    need = {}
                    for d in deps:
                        if d is None:
                            continue
                        if d[0] == "E" and d[1] == ename and ename == "pe":
                            continue
                        dk = (d[0], d[1])
                        if d[2] > need.get(dk, 0):
                            need[dk] = d[2]
                    for dk, val in need.items():
                        if seen.get(dk, 0) >= val:
                            continue
                        seen[dk] = val
                        sem = sems[dk[1]] if dk[0] == "E" else dsems[dk[1]]
                        eng.wait_ge(sem, val)
                    if kind == "wait":
                        continue
                    inst = fn(eng)
                    if kind == "op":
                        inst.then_inc(sems[ename], 1)
                    elif kind == "dma":
                        inst.then_inc(dsems[key], 16)
            return body

        with nc.Block() as block:
            block.tensor(make("pe"))
            block.scalar(make("act"))
            block.vector(make("dve"))
            block.gpsimd(make("pool"))
            block.sync(make("sp"))


S = 2048
D = 1024
NT = S // 128
NG = S // 512
KD = D // 128
NMETA = 16
WA = 512
DFF = 2816
NF = DFF // 128
EIN = 2560
EPS_RMS = 1e-6
EPS_LN = 1e-5
GOFF = 30 + NMETA
NWR = 6


def build_nc(debug=False):
    nc = bass.Bass("TRN2", target_bir_lowering=False)

    def din(name, shape):
        return nc.dram_tensor(name, list(shape), F32, kind="ExternalInput").ap()

    x_d = din("x", [S, D])
    meta_d = din("meta", [NMETA, D])
    gains_d = din("gains", [4, D])
    par_d = din("par", [37, WA])
    wrep_d = din("wrep", [128, 4 * 32])
    w_in_d = din("w_in", [D, EIN])
    w_out_d = din("w_out", [D, D])
    w_gate_d = din("w_gate", [D, DFF])
    w_up_d = din("w_up", [D, DFF])
    w_down_d = din("w_down", [DFF, D])
    out_d = nc.dram_tensor("out", [S, D], F32, kind="ExternalOutput").ap()
    if debug:
        d_xnT = nc.dram_tensor("d_xnT", [128, KD * (NMETA + S)], BF16, kind="ExternalOutput").ap()
        d_zb = nc.dram_tensor("d_zb", [128, 4 * S], F32, kind="ExternalOutput").ap()
        d_y = nc.dram_tensor("d_y", [128, 8 * S], BF16, kind="ExternalOutput").ap()
        d_h1 = nc.dram_tensor("d_h1", [128, NT * D], F32, kind="ExternalOutput").ap()
        d_parT = nc.dram_tensor("d_parT", [128, 148], F32, kind="ExternalOutput").ap()

    w_in_v = w_in_d.rearrange("(k p) e -> p k e", p=128)
    w_gate_v = w_gate_d.rearrange("(k p) e -> p k e", p=128)
    w_up_v = w_up_d.rearrange("(k p) e -> p k e", p=128)
    w_out_v = w_out_d.rearrange("(k p) e -> p k e", p=128)
    w_down_v = w_down_d.rearrange("(f p) e -> p f e", p=128)

    P = Prog()
    root = contextlib.ExitStack()

    def sb(stack, name, shape, dt, side=None):
        return stack.enter_context(nc.sbuf_tensor("sb_" + name, list(shape), dt, side=side))

    ident_f = sb(root, "ident_f", [128, 128], F32)
    ident_b = sb(root, "ident_b", [128, 128], BF16)
    ones_f = sb(root, "ones_f", [128, 128], F32)
    ones_src = sb(root, "ones_src", [128, 128], F32)
    parT = sb(root, "parT", [128, 4, 37], F32)
    stats = sb(root, "stats", [128, 272], F32)
    gain_a = sb(root, "gain_a", [128, D], F32)
    gain_b = sb(root, "gain_b", [128, D], F32)
    xn = [sb(root, "xn%d" % i, [128, D], BF16) for i in range(2)]
    wring = [sb(root, "wr%d" % i, [128, KD, 128], BF16) for i in range(NWR)]
    banks = [root.enter_context(nc.psum_tensor("bank%d" % i, [128, 512], F32)) for i in range(8)]

    nbank = [0]
    bank_pool = [list(range(8))]

    def bank():
        pool = bank_pool[0]
        b = pool[nbank[0] % len(pool)]
        nbank[0] += 1
        return b

    X_SS, X_STD, X_RS = 0, 17, 34
    M_SH, M_ST, M_STD, M_RS = 51, 83, 99, 115
    F_SS, F_STD, F_RS = 131, 147, 163
    G_SH, G_ST, G_STD, G_RS = 179, 211, 227, 243

    def scol(c):
        return stats[:, c:c + 1]

    chunk_list = []
    for cc in range(4):
        for base in (1536, 2048):
            chunk_list.append((w_in_v, base + cc * 128))
    for cc in range(4):
        for base in (512, 1024, 0):
            chunk_list.append((w_in_v, base + cc * 128))
    for hf in range(2):
        for f in range(NF):
            chunk_list.append((w_gate_v, f * 128))
            chunk_list.append((w_up_v, f * 128))
    loaded = [0]

    def prefetch(upto):
        upto = min(upto, len(chunk_list))
        while loaded[0] < upto:
            n = loaded[0]
            view, off = chunk_list[n]
            slot = n % NWR
            gate = [xt_ticket[NT // 2 - 1]] if (2 <= n < 8 and (NT // 2 - 1) in xt_ticket) else []
            P.dma("pool", lambda e, view=view, off=off, slot=slot: e.dma_start(
                out=wring[slot][:, :, :], in_=view[:, :, off:off + 128]),
                ("wr", slot), w=[("wr", slot)], extra=gate)
            loaded[0] += 1

    mixl = contextlib.ExitStack()
    m1 = contextlib.ExitStack()
    xn2T = sb(root, "xn2T", [128, KD, 1024], BF16)
    y = sb(mixl, "y", [128, 8, S], BF16)
    w_out = sb(mixl, "w_out", [128, KD, D], BF16)

    R = "right"
    xnT = sb(m1, "xnT", [128, KD, NMETA + S], BF16, R)
    zb = sb(m1, "zb", [128, 4, S], F32, R)
    stage = w_out[0:37, 3, :].bitcast(F32)
    meta_sb = w_out[0:NMETA, 0:2, :].rearrange("p k d -> p (k d)").bitcast(F32)
    xn_meta = w_out[0:NMETA, 2, :]
    junk_x = w_out[:, 4, :]
    zsq = [sb(m1, "zsq%d" % i, [128, 512], F32, R) for i in range(2)]
    GW = GOFF + S + 2
    g_bf = [sb(m1, "g_bf%d" % i, [128, GW], BF16, R) for i in range(2)]
    WG = S + 28
    grep = sb(m1, "grep", [128, 4, GW], BF16, R)
    grep_y = y[:, :, :].rearrange("p k t -> p (k t)")[:, 0:4 * GW].rearrange("p (b j) -> p b j", b=4)
    grep_bufs = [grep_y, grep]
    Wall = [sb(m1, "Wall%d" % i, [128, 32, 32], BF16, R) for i in range(2)]
    wrep_sb = sb(m1, "wrep_sb", [128, 4, 32], F32, R)
    Jt = sb(m1, "Jt", [128, 64], F32, R)
    Jb = sb(m1, "Jb", [128, 32], BF16, R)
    ca_meta = sb(m1, "ca_meta", [128, 4, NMETA], F32, R)
    tmpA = [sb(m1, "tmpA%d" % i, [128, 512], F32, R) for i in range(2)]
    tmpB = [sb(m1, "tmpB%d" % i, [128, 512], F32, R) for i in range(2)]
    mv = sb(m1, "mv", [128, 512], F32, R)
    ystage = y[:, :, :].rearrange("p k t -> p (k t)").bitcast(F32).rearrange("p (s j d) -> p s j d", s=4, j=2)
    NXS = 4
    xn_x = [xn[0][:, :], xn[1][:, :], w_out[:, 5, :], w_out[:, 6, :]]

    nA = [0]
    nB = [0]

    def nextA():
        s = nA[0] % 2
        nA[0] += 1
        return s

    def nextB():
        s = nB[0] % 2
        nB[0] += 1
        return s

    x_pairs = x_d.rearrange("(q j p) d -> q p j d", j=2, p=128)
    xt_ticket = {}

    def load_xt(q, eng="sp"):
        s = q % NXS
        xt_ticket[q] = P.dma(eng, lambda e, q=q, s=s: e.dma_start(out=ystage[:, s, :, :], in_=x_pairs[q]),
                             ("xt", s), w=[("xt", s)])

    P.dma("sp", lambda e: e.dma_start(out=stage[:, :], in_=par_d), "setup_par", w=["stage"])
    P.dma("sp", lambda e: e.dma_start(out=meta_sb[:, :], in_=meta_d), "setup_meta", w=["meta_sb"])
    load_xt(0)
    P.dma("sp", lambda e: e.dma_start(out=gain_a[:, :], in_=gains_d[0:1, :].partition_broadcast(128)), "ga", w=["ga"])
    for q in range(1, NXS):
        load_xt(q, "act" if q % 2 == 1 else "sp")
    prefetch(2)
    P.op("pool", lambda e: e.memset(ident_f[:, :], 1.0), w=["ident_f"])
    P.op("pool", lambda e: e.affine_select(out=ident_f[:, :], in_=ident_f[:, :], pattern=[[-1, 128]],
                                           compare_op=ALU.is_equal, fill=0.0, base=0, channel_multiplier=1),
         r=["ident_f"], w=["ident_f"])
    P.op("pool", lambda e: e.memset(ones_src[:, :], 1.0 / WA), w=["ones_src"])
    P.op("dve", lambda e: e.tensor_copy(ones_f[:, :].bitcast(F32R), ones_src[:, :]), r=["ones_src"], w=["ones_f"])
    for i in range(2):
        P.op("pool", lambda e, i=i: e.memset(g_bf[i][:, 0:30], 0.0), w=[("g", i, "z")])
        P.op("pool", lambda e, i=i: e.memset(g_bf[i][:, GOFF + S:GW], 0.0), w=[("g", i, "z2")])
    P.op("dve", lambda e: e.tensor_copy(ident_b[:, :], ident_f[:, :]), r=["ident_f"], w=["ident_b"])
    bp = bank()
    for cc in range(4):
        P.op("pe", lambda e, cc=cc: e.transpose(banks[bp][:, cc * 37:(cc + 1) * 37],
                                                stage[0:37, cc * 128:(cc + 1) * 128], ident_f[0:37, 0:37]),
             r=["stage", "ident_f"], w=[("bank", bp)])
    P.op("act", lambda e: e.activation(out=parT[:, :, :],
                                       in_=banks[bp][:, 0:148].rearrange("p (c k) -> p c k", c=4), func=AF.Copy),
         r=[("bank", bp)], w=["parT"])

    def norm_tile(src_ap, np_, ss_c, std_c, rs_c, r_src, xn_t, xn_key, gain, gain_key, junk=None):
        jt, jkey = (xn_t, xn_key) if junk is None else (junk, "junk_x")
        P.op("act", lambda e: e.activation(out=jt[0:np_, :], in_=src_ap, func=AF.Square,
                                           accum_out=stats[0:np_, ss_c:ss_c + 1]),
             r=r_src, w=[jkey, ("st", ss_c)])
        P.op("act", lambda e: e.activation(out=stats[0:np_, std_c:std_c + 1], in_=stats[0:np_, ss_c:ss_c + 1],
                                           func=AF.Sqrt, scale=1.0 / D, bias=EPS_RMS),
             r=[("st", ss_c)], w=[("st", std_c)])
        P.op("dve", lambda e: e.reciprocal(stats[0:np_, rs_c:rs_c + 1], stats[0:np_, std_c:std_c + 1]),
             r=[("st", std_c)], w=[("st", rs_c)])
        P.op("dve", lambda e: e.scalar_tensor_tensor(out=xn_t[0:np_, :], in0=src_ap, scalar=stats[0:np_, rs_c:rs_c + 1],
                                                     in1=gain[0:np_, :], op0=ALU.mult, op1=ALU.mult),
             r=list(r_src) + [("st", rs_c), gain_key], w=[xn_key])

    def transpose_tile(xn_t, np_, xn_key, dst_fn, dst_key, extra=()):
        b = bank()
        tp = banks[b][:, :].bitcast(BF16).rearrange("p (k t) -> p k t", k=KD)
        for k in range(KD):
            P.op("pe", lambda e, k=k: e.transpose(tp[:, k, 0:np_], xn_t[0:np_, k * 128:(k + 1) * 128],
                                                  ident_b[0:np_, 0:np_]),
                 r=[xn_key, "ident_b"], w=[("bank", b)])
        P.op("dve", lambda e: e.tensor_copy(dst_fn(), tp[:, :, 0:np_]), r=[("bank", b)], w=[dst_key], extra=extra)

    norm_tile(meta_sb[:, :], NMETA, X_SS + 16, X_STD + 16, X_RS + 16, ["meta_sb"], xn_meta, "xn_meta", gain_a, "ga")
    transpose_tile(xn_meta, NMETA, "xn_meta", lambda: xnT[:, :, 0:NMETA], ("xnT", "m"))
    def stage_x_norm(i):
        s = i % 4
        q = i // 2
        sx = q % NXS
        norm_tile(ystage[:, sx, i % 2, :], 128, X_SS + i, X_STD + i, X_RS + i, [("xt", sx)], xn_x[s], ("xn", s), gain_a, "ga",
                  junk=junk_x)
        if i % 2 == 1 and q + NXS < NT // 2:
            load_xt(q + NXS)

    def stage_x_tr(i):
        s = i % 4
        transpose_tile(xn_x[s], 128, ("xn", s),
                       lambda i=i: xnT[:, :, NMETA + i * 128:NMETA + (i + 1) * 128], ("xnT", i // 4))
    def load_gain_pre_ffn():
        P.dma("sp", lambda e: e.dma_start(out=gain_a[:, :], in_=gains_d[2:3, :].partition_broadcast(128)), "ga", w=["ga"])
        P.dma("sp", lambda e: e.dma_start(out=gain_b[:, :], in_=gains_d[1:2, :].partition_broadcast(128)), "gb", w=["gb"])
    def load_w_out():
        for hh in range(4):
            P.dma("pool", lambda e, hh=hh: e.dma_start(out=w_out[:, hh * 2:(hh + 1) * 2, :],
                                                      in_=w_out_v[:, hh * 2:(hh + 1) * 2, :]), "wo", w=["wo"],
                  extra=fence_x)

    chunk_i = [0]

    def next_chunk():
        n = chunk_i[0]
        chunk_i[0] += 1
        prefetch(n + 4)
        return n % NWR

    def proj(slot, act_T, act_key, col0, ncol, b=None):
        if b is None:
            b = bank()
        fns = [(lambda e, k=k: e.matmul(banks[b][:, 0:ncol], wring[slot][:, k, :], act_T[:, k, col0:col0 + ncol],
                                        start=(k == 0), stop=(k == KD - 1))) for k in range(KD)]
        P.group("pe", fns, r=[("wr", slot), act_key], w=[("bank", b)])
        return b

    ca_bf = [sb(m1, "ca_bf%d" % i, [128, 514], BF16, R) for i in range(2)]
    diag3 = [sb(m1, "diag3_%d" % i, [128, 3, 128], BF16, R) for i in range(2)]

    a_iter = [0]
    ln_enabled = [True]

    def path_A(cc):
        sc, sh, sbb = next_chunk(), next_chunk(), next_chunk()
        d3 = cc % 2
        P.op("dve", lambda e: e.tensor_tensor(
            out=diag3[d3][:, :, :], in0=ident_b[:, :].unsqueeze(1).broadcast_to([128, 3, 128]),
            in1=parT[:, cc, 0:3].unsqueeze(2).broadcast_to([128, 3, 128]), op=ALU.mult),
            r=["ident_b", "parT"], w=[("diag3", d3)])
        bc = proj(sc, xnT, ("xnT", "m"), 0, NMETA)
        bh = proj(sh, xnT, ("xnT", "m"), 0, NMETA)
        a = nextA()
        P.op("act", lambda e, bc=bc, a=a: e.activation(out=tmpA[a][:, 0:NMETA], in_=banks[bc][:, 0:NMETA], func=AF.Copy),
             r=[("bank", bc)], w=[("tmpA", a)])
        P.op("dve", lambda e, bh=bh, a=a: e.tensor_tensor(out=ca_meta[:, cc, :], in0=banks[bh][:, 0:NMETA],
                                                       in1=tmpA[a][:, 0:NMETA], op=ALU.mult),
             r=[("bank", bh), ("tmpA", a)], w=[("ca_meta", cc)])
        def ch(tg):
            col0 = NMETA + tg * 512
            bc = proj(sc, xnT, ("xnT", tg), col0, 512)
            bh = proj(sh, xnT, ("xnT", tg), col0, 512)
            a = nextA()
            cs = (cc * NG + tg) % 2
            P.op("act", lambda e, bc=bc, a=a: e.activation(out=tmpA[a][:, :], in_=banks[bc][:, :], func=AF.Copy),
                 r=[("bank", bc)], w=[("tmpA", a)])
            P.op("dve", lambda e, bh=bh, a=a, cs=cs: e.tensor_tensor(out=ca_bf[cs][:, 2:514], in0=banks[bh][:, :],
                                                                  in1=tmpA[a][:, :], op=ALU.mult),
                 r=[("bank", bh), ("tmpA", a)], w=[("ca", cs)])
            if tg == 0:
                P.op("dve", lambda e, cs=cs: e.tensor_copy(ca_bf[cs][:, 0:2], ca_meta[:, cc, NMETA - 2:NMETA]),
                     r=[("ca_meta", cc)], w=[("cah", cs)])
            else:
                P.op("dve", lambda e, cs=cs: e.tensor_copy(ca_bf[cs][:, 0:2], ca_bf[1 - cs][:, 512:514]),
                     r=[("ca", 1 - cs)], w=[("cah", cs)])

        ch(0)
        for tg in range(NG):
            col0 = NMETA + tg * 512
            cs = (cc * NG + tg) % 2
            if tg + 1 < NG:
                ch(tg + 1)
            bb = proj(sbb, xnT, ("xnT", tg), col0, 512)
            b3 = bank()
            fns = [(lambda e, k=k, b3=b3, cs=cs: e.matmul(banks[b3][:, :], diag3[d3][:, k, :], ca_bf[cs][:, k:k + 512],
                                                          start=(k == 0), stop=(k == 2))) for k in range(3)]
            P.group("pe", fns, r=[("diag3", d3), ("ca", cs), ("cah", cs)], w=[("bank", b3)])
            tb = nextB()
            P.op("act", lambda e, b3=b3, tb=tb: e.activation(out=tmpB[tb][:, :], in_=banks[b3][:, :], func=AF.Copy),
                 r=[("bank", b3)], w=[("tmpB", tb)])
            P.op("dve", lambda e, bb=bb, tb=tb, tg=tg: e.tensor_tensor(
                out=y[:, cc, tg * 512:(tg + 1) * 512], in0=banks[bb][:, :], in1=tmpB[tb][:, :], op=ALU.mult),
                r=[("bank", bb), ("tmpB", tb)], w=[("y", tg)], extra=fence_x)
            if ln_enabled[0]:
                a_iter[0] += 1
                ln_advance(NG + (a_iter[0] * 16 + 11) // 12)

    acc = [sb(m1, "acc_%d" % i, [128, 512], F32, R) for i in range(2)]

    P.dma("sp", lambda e: e.dma_start(out=wrep_sb[:, :, :].rearrange("p c k -> p (c k)"), in_=wrep_d), "setup_wrep",
          w=["wrep"])
    P.op("dve", lambda e: e.tensor_tensor(out=Jt[:, 0:32], in0=ident_f[:, 0:32], in1=ident_f[:, 32:64], op=ALU.add),
         r=["ident_f"], w=["Jt0"])
    P.op("dve", lambda e: e.tensor_tensor(out=Jt[:, 32:64], in0=ident_f[:, 64:96], in1=ident_f[:, 96:128], op=ALU.add),
         r=["ident_f"], w=["Jt1"])
    P.op("dve", lambda e: e.tensor_tensor(out=Jb[:, :], in0=Jt[:, 0:32], in1=Jt[:, 32:64], op=ALU.add),
         r=["Jt0", "Jt1"], w=["Jb"])

    glu_slots = {}

    def B_glu_slots(cc):
        if cc not in glu_slots:
            glu_slots[cc] = (next_chunk(), next_chunk())
        return glu_slots[cc]

    def B_glu_proj(cc, tg, bv=None, bg=None):
        sv, sg_ = B_glu_slots(cc)
        col0 = NMETA + tg * 512
        bv = proj(sv, xnT, ("xnT", tg), col0, 512, b=bv)
        bg = proj(sg_, xnT, ("xnT", tg), col0, 512, b=bg)
        return bv, bg

    def B_glu_epi(cc, tg, bv, bg):
        gs = cc % 2
        a = nextA()
        P.op("act", lambda e, bg=bg, a=a: e.activation(out=tmpA[a][:, :], in_=banks[bg][:, :], func=AF.Sigmoid),
             r=[("bank", bg)], w=[("tmpA", a)])
        P.op("dve", lambda e, bv=bv, a=a, tg=tg: e.tensor_tensor(
            out=g_bf[gs][:, GOFF + tg * 512:GOFF + (tg + 1) * 512], in0=banks[bv][:, :], in1=tmpA[a][:, :], op=ALU.mult),
            r=[("bank", bv), ("tmpA", a)], w=[("g", gs, tg)])

    def B_glu(cc, parked=()):
        sv, sg_ = B_glu_slots(cc)
        gs = cc % 2
        bv = proj(sv, xnT, ("xnT", "m"), 0, NMETA)
        bg = proj(sg_, xnT, ("xnT", "m"), 0, NMETA)
        a = nextA()
        P.op("act", lambda e, bg=bg, a=a: e.activation(out=tmpA[a][:, 0:NMETA], in_=banks[bg][:, 0:NMETA], func=AF.Sigmoid),
             r=[("bank", bg)], w=[("tmpA", a)])
        P.op("dve", lambda e, bv=bv, a=a: e.tensor_tensor(out=g_bf[gs][:, 30:GOFF], in0=banks[bv][:, 0:NMETA],
                                                       in1=tmpA[a][:, 0:NMETA], op=ALU.mult),
             r=[("bank", bv), ("tmpA", a)], w=[("g", gs, "m")])
        for tg in range(NG):
            if tg < len(parked):
                bv, bg = parked[tg]
            else:
                bv, bg = B_glu_proj(cc, tg)
            B_glu_epi(cc, tg, bv, bg)

    def B_rep(cc):
        gs = cc % 2
        gb = grep_bufs[cc % 2]
        gkeys = [("g", gs, t) for t in range(NG)] + [("g", gs, "m"), ("g", gs, "z"), ("g", gs, "z2")]
        for b in range(4):
            for r in range(4):
                P.dma("sp",
                      lambda e, b=b, r=r: e.dma_start(out=gb[32 * r:32 * r + 32, b, 0:WG],
                                                           in_=g_bf[gs][32 * b:32 * b + 32, 16 + r:16 + r + WG]),
                      ("rep", cc % 2, b, r), r=gkeys, w=[("grep", cc % 2, b, r)],
                      extra=fence_x)

    def B_wall(cc):
        wl = cc % 2
        P.op("dve", lambda e: e.tensor_tensor(
            out=Wall[wl][:, :, :], in0=Jb[:, :].unsqueeze(1).broadcast_to([128, 32, 32]),
            in1=wrep_sb[:, cc, :].unsqueeze(2).broadcast_to([128, 32, 32]), op=ALU.mult),
            r=["Jb", "wrep"], w=[("Wall", wl)])

    def B_conv(cc):
        wl = cc % 2
        gb = grep_bufs[cc % 2]
        for tg in range(NG):
            bz = bank()
            fns = [(lambda e, q=q, b=b, bz=bz, tg=tg: e.matmul(
                banks[bz][32 * b:32 * b + 32, :], Wall[wl][:, q * 4 + b, :],
                gb[:, b, tg * 512 + 4 * q:tg * 512 + 4 * q + 512],
                start=(q == 0), stop=(q == 7), tile_position=(0, 32 * b))) for q in range(8) for b in range(4)]
            P.group("pe", fns, r=[("Wall", wl)] + [("grep", cc % 2, b, r) for b in range(4) for r in range(4)],
                    w=[("bank", bz)])
            P.op("act", lambda e, bz=bz, tg=tg: e.activation(
                out=zb[:, cc, tg * 512:(tg + 1) * 512].bitcast(F32R), in_=banks[bz][:, :], func=AF.Identity,
                bias=parT[:, cc, 34:35]),
                r=[("bank", bz), "parT"], w=[("zb", cc, tg)], extra=fence_x)

    _dv = grep[:, :, :].rearrange("p b j -> p (b j)").bitcast(F32)
    _gv = [g[:, :].bitcast(F32) for g in g_bf]
    mean_v = [_dv[:, t * 512:(t + 1) * 512] for t in range(NG)]
    rstd_v = [_gv[t // 2][:, (t % 2) * 512:(t % 2 + 1) * 512] for t in range(NG)]
    fence_B = []
    ln_parts = []

    def ln_stats(tg):
        cols = slice(tg * 512, (tg + 1) * 512)
        b1 = bank()
        b2 = bank()
        fns = [(lambda e, cc=cc: e.matmul(banks[b1][:, :], ones_f[:, :].bitcast(F32R), zb[:, cc, cols].bitcast(F32R),
                                          start=(cc == 0), stop=(cc == 3))) for cc in range(4)]

        def sq(cc):
            a = cc % 2
            P.op("act", lambda e: e.activation(out=zsq[a][:, :].bitcast(F32R), in_=zb[:, cc, cols], func=AF.Square),
                 r=[("zb", cc, tg)], w=[("zsq", a)])

        sq(0)
        sq(1)
        P.group("pe", fns, r=["ones_f"] + [("zb", cc, tg) for cc in range(4)], w=[("bank", b1)])
        for cc in range(4):
            a = cc % 2
            P.op("pe", lambda e, a=a, cc=cc: e.matmul(banks[b2][:, :], ones_f[:, :].bitcast(F32R), zsq[a][:, :].bitcast(F32R),
                                                      start=(cc == 0), stop=(cc == 3)),
                 r=["ones_f", ("zsq", a)], w=[("bank", b2)])
            if cc + 2 < 4:
                sq(cc + 2)
        P.op("act", lambda e: e.activation(out=mean_v[tg], in_=banks[b1][:, :], func=AF.Copy),
             r=[("bank", b1)], w=[("mean", tg)], extra=fence_B)
        P.op("act", lambda e: e.activation(out=mv[:, :], in_=banks[b1][:, :], func=AF.Square),
             r=[("bank", b1)], w=["mv"])
        P.op("dve", lambda e: e.tensor_tensor(out=mv[:, :], in0=banks[b2][:, :], in1=mv[:, :], op=ALU.subtract),
             r=[("bank", b2), "mv"], w=["mv"])
        P.op("act", lambda e: e.activation(out=rstd_v[tg], in_=mv[:, :], func=AF.Sqrt, scale=1.0, bias=EPS_LN),
             r=["mv"], w=[("rstd", tg)], extra=fence_B)
        P.op("dve", lambda e: e.reciprocal(rstd_v[tg], rstd_v[tg]), r=[("rstd", tg)], w=[("rstd", tg)])

    def ln_norm(tg, cc):
        cols = slice(tg * 512, (tg + 1) * 512)
        tb = (tg * 4 + cc) % 2
        P.op("dve", lambda e: e.tensor_tensor(out=acc[tb][:, :], in0=zb[:, cc, cols], in1=mean_v[tg], op=ALU.subtract),
             r=[("zb", cc, tg), ("mean", tg)], w=[("acc", tb)])
        P.op("dve", lambda e: e.tensor_tensor(out=acc[tb][:, :], in0=acc[tb][:, :], in1=rstd_v[tg], op=ALU.mult),
             r=[("acc", tb), ("rstd", tg)], w=[("acc", tb)])
        P.op("act", lambda e: e.activation(out=y[:, 4 + cc, cols], in_=acc[tb][:, :], func=AF.Silu,
                                           scale=parT[:, cc, 35:36], bias=parT[:, cc, 36:37]),
             r=[("acc", tb), "parT"], w=[("y", tg)], extra=fence_x)

    for tg_ in range(NG):
        ln_parts.append(lambda tg_=tg_: ln_stats(tg_))
    for tg_ in range(NG):
        for cc_ in range(4):
            ln_parts.append(lambda tg_=tg_, cc_=cc_: ln_norm(tg_, cc_))
    ln_done = [0]

    def ln_advance(upto):
        while ln_done[0] < min(upto, len(ln_parts)):
            ln_parts[ln_done[0]]()
            ln_done[0] += 1

    fence_x = []

    parked0 = []

    def stage_x_all():
        for i in range(3):
            stage_x_norm(i)
        for i in range(NT):
            if i == 9:
                bank_pool[0] = [0, 1]
                nbank[0] = 0
            stage_x_tr(i)
            if i + 3 < NT:
                stage_x_norm(i + 3)
            if i + 3 == NT - 1:
                P.op("act", lambda e: e.activation(out=stats[:, 268:269], in_=stats[:, X_RS:X_RS + 1], func=AF.Sigmoid),
                     r=[("st", X_RS)], w=[("st", 268)])
            if i in (10, 12, 14):
                t = (i - 10) // 2
                parked0.append(B_glu_proj(0, t, bv=2 + 2 * t, bg=3 + 2 * t))

    for cc in range(4):
        if cc == 0:
            stage_x_all()
            load_gain_pre_ffn()
            fence_x.extend(P.fence())
        if cc == 0:
            B_wall(0)
            B_glu(0, parked=parked0)
            bank_pool[0] = list(range(8))
            B_rep(0)
        if cc + 1 < 4:
            B_wall(cc + 1)
            B_glu(cc + 1)
            B_rep(cc + 1)
        if cc == 3:
            ln_enabled[0] = False
            path_A(0)
            ln_enabled[0] = True
        B_conv(cc)
        if cc == 2:
            fence_x.extend(P.fence())
        if cc == 1:
            load_w_out()
    fence_B.extend(P.fence())
    ln_advance(NG)
    for cc in range(1, 4):
        path_A(cc)
    ln_advance(len(ln_parts))

    if debug:
        P.dma("sp", lambda e: e.dma_start(out=d_xnT, in_=xnT[:, :, :].rearrange("p k t -> p (k t)")), "dbg",
              r=[("xnT", "m")] + [("xnT", t) for t in range(NG)])
        P.dma("sp", lambda e: e.dma_start(out=d_zb, in_=zb[:, :, :].rearrange("p k t -> p (k t)")), "dbg",
              r=[("zb", c, t) for c in range(4) for t in range(NG)])
        P.dma("sp", lambda e: e.dma_start(out=d_y, in_=y[:, :, :].rearrange("p k t -> p (k t)")), "dbg",
              r=[("y", t) for t in range(NG)])
        P.dma("sp", lambda e: e.dma_start(out=d_parT, in_=parT[:, :, :].rearrange("p k t -> p (k t)")), "dbg", r=["parT"])
    fence1 = P.fence()
    m1.close()
    hstack = contextlib.ExitStack()
    h = sb(hstack, "h", [128, NT, D], F32, R)
    t1_tok = sb(hstack, "t1_tok", [128, D], F32, R)

    def post_norm_residual(i, bks, SH, ST, STD, RS, gain, gain_key, first=False, extra=()):
        if first:
            dst = lambda dh: h[:, i, dh * 512:(dh + 1) * 512]
            dkey = lambda dh: ("h", i)
        else:
            dst = lambda dh: t1_tok[:, dh * 512:(dh + 1) * 512]
            dkey = lambda dh: ("t1", dh)
        for dh in range(2):
            P.op("act", lambda e, dh=dh: e.activation(out=dst(dh), in_=banks[bks[dh]][:, :],
                                                     func=AF.Square, accum_out=scol(SH + 2 * i + dh)),
                 r=[("bank", bks[dh])], w=[dkey(dh), ("st", SH + 2 * i + dh)], extra=extra)
        P.op("dve", lambda e: e.tensor_tensor(out=scol(ST + i), in0=scol(SH + 2 * i), in1=scol(SH + 2 * i + 1), op=ALU.add),
             r=[("st", SH + 2 * i), ("st", SH + 2 * i + 1)], w=[("st", ST + i)])
        P.op("act", lambda e: e.activation(out=scol(STD + i), in_=scol(ST + i), func=AF.Sqrt, scale=1.0 / D, bias=EPS_RMS),
             r=[("st", ST + i)], w=[("st", STD + i)])
        P.op("dve", lambda e: e.reciprocal(scol(RS + i), scol(STD + i)), r=[("st", STD + i)], w=[("st", RS + i)])
        for dh in range(2):
            P.op("dve", lambda e, dh=dh: e.scalar_tensor_tensor(
                out=dst(dh), in0=banks[bks[dh]][:, :], scalar=scol(RS + i),
                in1=gain[:, dh * 512:(dh + 1) * 512], op0=ALU.mult, op1=ALU.mult),
                r=[("bank", bks[dh]), ("st", RS + i), gain_key], w=[dkey(dh)])
        if first:
            P.dma("pool", lambda e: e.dma_start(out=h[:, i, :], in_=x_d[i * 128:(i + 1) * 128, :], accum_op=ALU.add),
                  ("hacc", i), r=[("h", i)], w=[("h", i)])
        else:
            P.op("dve", lambda e: e.tensor_tensor(out=h[:, i, :], in0=h[:, i, :], in1=t1_tok[:, :], op=ALU.add),
                 r=[("t1", 0), ("t1", 1), ("h", i)], w=[("h", i)])

    def xn2_norm(hf, j):
        i = hf * 8 + j
        s = i % 2
        norm_tile(h[:, i, :], 128, F_SS + i, F_STD + i, F_RS + i, [("h", i)], xn[s], ("xn", s), gain_a, "ga")

    def xn2_tr(hf, j):
        i = hf * 8 + j
        s = i % 2
        transpose_tile(xn[s], 128, ("xn", s), lambda j=j: xn2T[:, :, j * 128:(j + 1) * 128], ("xn2T", j // 4))

    prefetch(chunk_i[0] + NWR)
    last_out_mm = [None]
    for i in range(NT):
        bks = []
        for dh in range(2):
            b = bank()
            fns = [(lambda e, k=k, b=b, dh=dh, i=i: e.matmul(banks[b][:, :], y[:, k, i * 128:(i + 1) * 128],
                                                        w_out[:, k, dh * 512:(dh + 1) * 512],
                                                        start=(k == 0), stop=(k == KD - 1))) for k in range(KD)]
            last_out_mm[0] = P.group("pe", fns, r=[("y", i // 4), "wo"], w=[("bank", b)])
            bks.append(b)
        post_norm_residual(i, bks, M_SH, M_ST, M_STD, M_RS, gain_b, "gb", first=True, extra=fence1)
        j = i - 4
        if 0 <= j < 8:
            xn2_norm(0, j)
            if j >= 1:
                xn2_tr(0, j - 1)
        if j == 8:
            xn2_tr(0, 7)
    P.dma("sp", lambda e: e.dma_start(out=gain_b[:, :], in_=gains_d[3:4, :].partition_broadcast(128)), "gb", w=["gb"])

    if debug:
        P.dma("sp", lambda e: e.dma_start(out=d_h1, in_=h[:, :, :].rearrange("p k t -> p (k t)")), "dbg",
              r=[("h", i) for i in range(NT)])
    fence2 = P.fence()
    fence_y = [last_out_mm[0]] + ([("D", "dbg", P.dcnt["dbg"])] if debug else [])
    mixl.close()
    ffn = contextlib.ExitStack()
    actb = sb(ffn, "actb", [128, NF, 1024], BF16)
    w_down = sb(ffn, "w_down", [128, NF, D], BF16)
    sring = [sb(ffn, "sring%d" % i, [128, 512], F32) for i in range(2)]
    def load_wd(f):
        P.dma("pool", lambda e, f=f: e.dma_start(out=w_down[:, f:f + 1, :], in_=w_down_v[:, f:f + 1, :]),
              "wd", w=["wd"], extra=fence1 + fence_y)

    def down_mm(hf, j):
        bks = []
        for dh in range(2):
            b = bank()
            fns = [(lambda e, f=f, b=b, dh=dh: e.matmul(banks[b][:, :], actb[:, f, j * 128:(j + 1) * 128],
                                                        w_down[:, f, dh * 512:(dh + 1) * 512],
                                                        start=(f == 0), stop=(f == NF - 1))) for f in range(NF)]
            P.group("pe", fns, r=[("act", j // 4), "wd"], w=[("bank", b)])
            bks.append(b)
        return bks

    def down_epi(hf, j, bks):
        i = hf * 8 + j
        post_norm_residual(i, bks, G_SH, G_ST, G_STD, G_RS, gain_b, "gb")
        P.dma("sp", lambda e: e.dma_start(out=out_d[i * 128:(i + 1) * 128, :], in_=h[:, i, :]), "out", r=[("h", i)])

    for hf in range(2):
        for f in range(NF):
            sg_, su_ = next_chunk(), next_chunk()
            if hf == 0:
                load_wd(f)
            for t2 in range(2):
                ba = proj(sg_, xn2T, ("xn2T", t2), t2 * 512, 512)
                bu = proj(su_, xn2T, ("xn2T", t2), t2 * 512, 512)
                sr = (f * 2 + t2) % 2
                P.op("act", lambda e, ba=ba, sr=sr: e.activation(out=sring[sr][:, :], in_=banks[ba][:, :], func=AF.Silu),
                     r=[("bank", ba)], w=[("sring", sr)], extra=fence1)
                P.op("dve", lambda e, bu=bu, sr=sr, f=f, t2=t2: e.tensor_tensor(
                    out=actb[:, f, t2 * 512:(t2 + 1) * 512], in0=banks[bu][:, :], in1=sring[sr][:, :], op=ALU.mult),
                    r=[("bank", bu), ("sring", sr)], w=[("act", t2)], extra=fence_y)
        if hf == 0:
            xn2_norm(1, 0)
        for j in range(8):
            bks = down_mm(hf, j)
            if hf == 0:
                xn2_tr(1, j)
                if j + 1 < 8:
                    xn2_norm(1, j + 1)
            down_epi(hf, j, bks)
    P.wait("sp", [("D", "out", P.dcnt["out"])] + ([("D", "dbg", P.dcnt["dbg"])] if debug else []))

    P.emit(nc, root)
    ffn.close()
    hstack.close()
    root.close()
    return nc


_NC_CACHE = {}


def kernel(**inputs):
    f32 = lambda a: np.ascontiguousarray(np.asarray(a), dtype=np.float32)
    x = f32(inputs["x"])
    B = x.shape[0]
    meta = f32(inputs["meta_tokens"])
    gains = np.concatenate([f32(inputs[k]).reshape(1, D) for k in
                            ("pre_mix_norm", "post_mix_norm", "pre_ffn_norm", "post_ffn_norm")], axis=0)
    par = np.concatenate([f32(inputs["conv_a_w"]).reshape(3, WA), f32(inputs["conv_b_w"]).reshape(31, WA),
                          f32(inputs["conv_b_bias"]).reshape(1, WA), f32(inputs["ln_b_gain"]).reshape(1, WA),
                          f32(inputs["ln_b_bias"]).reshape(1, WA)], axis=0)
    wb32 = np.concatenate([f32(inputs["conv_b_w"]).reshape(31, WA), np.zeros((1, WA), np.float32)], axis=0)
    wrep = np.ascontiguousarray(wb32.reshape(8, 4, 4, 4, 32).transpose(1, 4, 2, 0, 3)).reshape(128, 128)
    shared = dict(meta=meta, gains=np.ascontiguousarray(gains), par=np.ascontiguousarray(par), wrep=wrep,
                  w_in=f32(inputs["w_in"]).reshape(D, EIN), w_out=f32(inputs["w_out"]).reshape(D, D),
                  w_gate=f32(inputs["w_gate"]).reshape(D, DFF), w_up=f32(inputs["w_up"]).reshape(D, DFF),
                  w_down=f32(inputs["w_down"]).reshape(DFF, D))
    if "nc" not in _NC_CACHE:
        _NC_CACHE["nc"] = build_nc()
    nc = _NC_CACHE["nc"]
    in_maps = [dict(shared, x=np.ascontiguousarray(x[b])) for b in range(B)]
    res = run_bass_kernel_spmd(nc, in_maps, core_ids=list(range(B)))
    return np.stack([np.asarray(r["out"], dtype=np.float32) for r in res.results], axis=0)
```

```python
import contextlib
import numpy as np
import concourse.bass as bass
import concourse.mybir as mybir
from concourse.bass_utils import run_bass_kernel_spmd

F32 = mybir.dt.float32
BF16 = mybir.dt.bfloat16
F32R = mybir.dt.float32r
AF = mybir.ActivationFunctionType
ALU = mybir.AluOpType

ENGS = ("pe", "act", "dve", "pool", "sp")


class Prog:
    def __init__(self):
        self.ops = {e: [] for e in ENGS}
        self.cnt = {e: 0 for e in ENGS}
        self.dcnt = {}
        self.lastw = {}
        self.reads = {}

    def _deps(self, r, w, extra):
        deps = list(extra)
        for k in r:
            if k in self.lastw:
                deps.append(self.lastw[k])
        for k in w:
            if k in self.lastw:
                deps.append(self.lastw[k])
            deps.extend(self.reads.get(k, ()))
        return deps

    def _commit(self, t, r, w):
        for k in r:
            self.reads.setdefault(k, []).append(t)
        for k in w:
            self.lastw[k] = t
            self.reads[k] = []

    def op(self, eng, fn, r=(), w=(), extra=()):
        deps = self._deps(r, w, extra)
        self.cnt[eng] += 1
        t = ("E", eng, self.cnt[eng])
        self.ops[eng].append(("op", fn, deps, None))
        self._commit(t, r, w)
        return t

    def group(self, eng, fns, r=(), w=(), extra=()):
        deps = self._deps(r, w, extra)
        self.ops[eng].append(("wait", None, deps, None))
        for f in fns[:-1]:
            self.ops[eng].append(("quiet", f, [], None))
        self.cnt[eng] += 1
        t = ("E", eng, self.cnt[eng])
        self.ops[eng].append(("op", fns[-1], [], None))
        self._commit(t, r, w)
        return t

    def fence(self):
        return [("E", e, c) for e, c in self.cnt.items() if c] + \
               [("D", k, c) for k, c in self.dcnt.items()]

    def quiet(self, eng, fn):
        self.ops[eng].append(("quiet", fn, [], None))

    def dma(self, eng, fn, key, r=(), w=(), extra=()):
        deps = self._deps(r, w, extra)
        self.dcnt[key] = self.dcnt.get(key, 0) + 16
        t = ("D", key, self.dcnt[key])
        self.ops[eng].append(("dma", fn, deps, key))
        self._commit(t, r, w)
        return t

    def wait(self, eng, deps):
        self.ops[eng].append(("wait", None, list(deps), None))

    def emit(self, nc, stack):
        sems = {e: stack.enter_context(nc.semaphore("s_" + e)) for e in ENGS}
        dsems = {k: stack.enter_context(nc.semaphore("d_%d" % i))
                 for i, k in enumerate(self.dcnt)}
        ops = self.ops

        def make(ename):
            def body(eng):
                seen = {}
                for kind, fn, deps, key in ops[ename]:
                    need = {}
                    for d in deps:
                        if d is None:
                            continue
                        if d[0] == "E" and d[1] == ename and ename == "pe":
                            continue
                        dk = (d[0], d[1])
                        if d[2] > need.get(dk, 0):
                            need[dk] = d[2]
                    for dk, val in need.items():
                        if seen.get(dk, 0) >= val:
                            continue
                        seen[dk] = val
                        sem = sems[dk[1]] if dk[0] == "E" else dsems[dk[1]]
                        eng.wait_ge(sem, val)
                    if kind == "wait":
                        continue
                    inst = fn(eng)
                    if kind == "op":
                        inst.then_inc(sems[ename], 1)
                    elif kind == "dma":
                        inst.then_inc(dsems[key], 16)
            return body

        with nc.Block() as block:
            block.tensor(make("pe"))
            block.scalar(make("act"))
            block.vector(make("dve"))
            block.gpsimd(make("pool"))
            block.sync(make("sp"))


S = 2048
D = 1024
NT = S // 128
NG = S // 512
KD = D // 128
NMETA = 16
WA = 512
DFF = 2816
NF = DFF // 128
EIN = 2560
EPS_RMS = 1e-6
EPS_LN = 1e-5
GOFF = 30 + NMETA
NWR = 6


def build_nc(debug=False):
    nc = bass.Bass("TRN2", target_bir_lowering=False)

    def din(name, shape):
        return nc.dram_tensor(name, list(shape), F32, kind="ExternalInput").ap()

    x_d = din("x", [S, D])
    meta_d = din("meta", [NMETA, D])
    gains_d = din("gains", [4, D])
    par_d = din("par", [37, WA])
    wrep_d = din("wrep", [128, 4 * 32])
    w_in_d = din("w_in", [D, EIN])
    w_out_d = din("w_out", [D, D])
    w_gate_d = din("w_gate", [D, DFF])
    w_up_d = din("w_up", [D, DFF])
    w_down_d = din("w_down", [DFF, D])
    out_d = nc.dram_tensor("out", [S, D], F32, kind="ExternalOutput").ap()
    if debug:
        d_xnT = nc.dram_tensor("d_xnT", [128, KD * (NMETA + S)], BF16, kind="ExternalOutput").ap()
        d_zb = nc.dram_tensor("d_zb", [128, 4 * S], F32, kind="ExternalOutput").ap()
        d_y = nc.dram_tensor("d_y", [128, 8 * S], BF16, kind="ExternalOutput").ap()
        d_h1 = nc.dram_tensor("d_h1", [128, NT * D], F32, kind="ExternalOutput").ap()
        d_parT = nc.dram_tensor("d_parT", [128, 148], F32, kind="ExternalOutput").ap()

    w_in_v = w_in_d.rearrange("(k p) e -> p k e", p=128)
    w_gate_v = w_gate_d.rearrange("(k p) e -> p k e", p=128)
    w_up_v = w_up_d.rearrange("(k p) e -> p k e", p=128)
    w_out_v = w_out_d.rearrange("(k p) e -> p k e", p=128)
    w_down_v = w_down_d.rearrange("(f p) e -> p f e", p=128)

    P = Prog()
    root = contextlib.ExitStack()

    def sb(stack, name, shape, dt, side=None):
        return stack.enter_context(nc.sbuf_tensor("sb_" + name, list(shape), dt, side=side))

    ident_f = sb(root, "ident_f", [128, 128], F32)
    ident_b = sb(root, "ident_b", [128, 128], BF16)
    ones_f = sb(root, "ones_f", [128, 128], F32)
    ones_src = sb(root, "ones_src", [128, 128], F32)
    parT = sb(root, "parT", [128, 4, 37], F32)
    stats = sb(root, "stats", [128, 272], F32)
    gain_a = sb(root, "gain_a", [128, D], F32)
    gain_b = sb(root, "gain_b", [128, D], F32)
    xn = [sb(root, "xn%d" % i, [128, D], BF16) for i in range(2)]
    wring = [sb(root, "wr%d" % i, [128, KD, 128], BF16) for i in range(NWR)]
    banks = [root.enter_context(nc.psum_tensor("bank%d" % i, [128, 512], F32)) for i in range(8)]

    nbank = [0]
    bank_pool = [list(range(8))]

    def bank():
        pool = bank_pool[0]
        b = pool[nbank[0] % len(pool)]
        nbank[0] += 1
        return b

    X_SS, X_STD, X_RS = 0, 17, 34
    M_SH, M_ST, M_STD, M_RS = 51, 83, 99, 115
    F_SS, F_STD, F_RS = 131, 147, 163
    G_SH, G_ST, G_STD, G_RS = 179, 211, 227, 243

    def scol(c):
        return stats[:, c:c + 1]

    chunk_list = []
    for cc in range(4):
        for base in (1536, 2048):
            chunk_list.append((w_in_v, base + cc * 128))
    for cc in range(4):
        for base in (512, 1024, 0):
            chunk_list.append((w_in_v, base + cc * 128))
    for hf in range(2):
        for f in range(NF):
            chunk_list.append((w_gate_v, f * 128))
            chunk_list.append((w_up_v, f * 128))
    loaded = [0]

    def prefetch(upto):
        upto = min(upto, len(chunk_list))
        while loaded[0] < upto:
            n = loaded[0]
            view, off = chunk_list[n]
            slot = n % NWR
            gate = [xt_ticket[NT // 2 - 1]] if (2 <= n < 8 and (NT // 2 - 1) in xt_ticket) else []
            P.dma("pool", lambda e, view=view, off=off, slot=slot: e.dma_start(
                out=wring[slot][:, :, :], in_=view[:, :, off:off + 128]),
                ("wr", slot), w=[("wr", slot)], extra=gate)
            loaded[0] += 1

    mixl = contextlib.ExitStack()
    m1 = contextlib.ExitStack()
    xn2T = sb(root, "xn2T", [128, KD, 1024], BF16)
    y = sb(mixl, "y", [128, 8, S], BF16)
    w_out = sb(mixl, "w_out", [128, KD, D], BF16)

    R = "right"
    xnT = sb(m1, "xnT", [128, KD, NMETA + S], BF16, R)
    zb = sb(m1, "zb", [128, 4, S], F32, R)
    stage = w_out[0:37, 3, :].bitcast(F32)
    meta_sb = w_out[0:NMETA, 0:2, :].rearrange("p k d -> p (k d)").bitcast(F32)
    xn_meta = w_out[0:NMETA, 2, :]
    junk_x = w_out[:, 4, :]
    zsq = [sb(m1, "zsq%d" % i, [128, 512], F32, R) for i in range(2)]
    GW = GOFF + S + 2
    g_bf = [sb(m1, "g_bf%d" % i, [128, GW], BF16, R) for i in range(2)]
    WG = S + 28
    grep = sb(m1, "grep", [128, 4, GW], BF16, R)
    grep_y = y[:, :, :].rearrange("p k t -> p (k t)")[:, 0:4 * GW].rearrange("p (b j) -> p b j", b=4)
    grep_bufs = [grep_y, grep]
    Wall = [sb(m1, "Wall%d" % i, [128, 32, 32], BF16, R) for i in range(2)]
    wrep_sb = sb(m1, "wrep_sb", [128, 4, 32], F32, R)
    Jt = sb(m1, "Jt", [128, 64], F32, R)
    Jb = sb(m1, "Jb", [128, 32], BF16, R)
    ca_meta = sb(m1, "ca_meta", [128, 4, NMETA], F32, R)
    tmpA = [sb(m1, "tmpA%d" % i, [128, 512], F32, R) for i in range(2)]
    tmpB = [sb(m1, "tmpB%d" % i, [128, 512], F32, R) for i in range(2)]
    mv = sb(m1, "mv", [128, 512], F32, R)
    ystage = y[:, :, :].rearrange("p k t -> p (k t)").bitcast(F32).rearrange("p (s j d) -> p s j d", s=4, j=2)
    NXS = 4
    xn_x = [xn[0][:, :], xn[1][:, :], w_out[:, 5, :], w_out[:, 6, :]]

    nA = [0]
    nB = [0]

    def nextA():
        s = nA[0] % 2
        nA[0] += 1
        return s

    def nextB():
        s = nB[0] % 2
        nB[0] += 1
        return s

    x_pairs = x_d.rearrange("(q j p) d -> q p j d", j=2, p=128)
    xt_ticket = {}

    def load_xt(q):
        s = q % NXS
        if q == NT // 2 - 1:
            P.dma("sp", lambda e, q=q, s=s: e.dma_start(out=ystage[:, s, 0, :], in_=x_pairs[q][:, 0, :]),
                  ("xt", s), w=[("xt", s), ("xtl", 0)])
            P.dma("sp", lambda e, q=q, s=s: e.dma_start(out=ystage[:, s, 1, :], in_=x_pairs[q][:, 1, :]),
                  "xt_last", w=[("xtl", 1)])
            return
        xt_ticket[q] = P.dma("sp", lambda e, q=q, s=s: e.dma_start(out=ystage[:, s, :, :], in_=x_pairs[q]),
                             ("xt", s), w=[("xt", s)])

    P.dma("sp", lambda e: e.dma_start(out=stage[:, :], in_=par_d), "setup_par", w=["stage"])
    P.dma("sp", lambda e: e.dma_start(out=meta_sb[:, :], in_=meta_d), "setup_meta", w=["meta_sb"])
    load_xt(0)
    P.dma("sp", lambda e: e.dma_start(out=gain_a[:, :], in_=gains_d[0:1, :].partition_broadcast(128)), "ga", w=["ga"])
    for q in range(1, NXS):
        load_xt(q)
    prefetch(2)
    P.op("pool", lambda e: e.memset(ident_f[:, :], 1.0), w=["ident_f"])
    P.op("pool", lambda e: e.affine_select(out=ident_f[:, :], in_=ident_f[:, :], pattern=[[-1, 128]],
                                           compare_op=ALU.is_equal, fill=0.0, base=0, channel_multiplier=1),
         r=["ident_f"], w=["ident_f"])
    P.op("pool", lambda e: e.memset(ones_src[:, :], 1.0 / WA), w=["ones_src"])
    P.op("dve", lambda e: e.tensor_copy(ones_f[:, :].bitcast(F32R), ones_src[:, :]), r=["ones_src"], w=["ones_f"])
    for i in range(2):
        P.op("pool", lambda e, i=i: e.memset(g_bf[i][:, 0:30], 0.0), w=[("g", i, "z")])
        P.op("pool", lambda e, i=i: e.memset(g_bf[i][:, GOFF + S:GW], 0.0), w=[("g", i, "z2")])
    P.op("dve", lambda e: e.tensor_copy(ident_b[:, :], ident_f[:, :]), r=["ident_f"], w=["ident_b"])
    bp = bank()
    for cc in range(4):
        P.op("pe", lambda e, cc=cc: e.transpose(banks[bp][:, cc * 37:(cc + 1) * 37],
                                                stage[0:37, cc * 128:(cc + 1) * 128], ident_f[0:37, 0:37]),
             r=["stage", "ident_f"], w=[("bank", bp)])
    P.op("act", lambda e: e.activation(out=parT[:, :, :],
                                       in_=banks[bp][:, 0:148].rearrange("p (c k) -> p c k", c=4), func=AF.Copy),
         r=[("bank", bp)], w=["parT"])

    def norm_tile(src_ap, np_, ss_c, std_c, rs_c, r_src, xn_t, xn_key, gain, gain_key, junk=None):
        jt, jkey = (xn_t, xn_key) if junk is None else (junk, "junk_x")
        P.op("act", lambda e: e.activation(out=jt[0:np_, :], in_=src_ap, func=AF.Square,
                                           accum_out=stats[0:np_, ss_c:ss_c + 1]),
             r=r_src, w=[jkey, ("st", ss_c)])
        P.op("act", lambda e: e.activation(out=stats[0:np_, std_c:std_c + 1], in_=stats[0:np_, ss_c:ss_c + 1],
                                           func=AF.Sqrt, scale=1.0 / D, bias=EPS_RMS),
             r=[("st", ss_c)], w=[("st", std_c)])
        P.op("dve", lambda e: e.reciprocal(stats[0:np_, rs_c:rs_c + 1], stats[0:np_, std_c:std_c + 1]),
             r=[("st", std_c)], w=[("st", rs_c)])
        P.op("dve", lambda e: e.scalar_tensor_tensor(out=xn_t[0:np_, :], in0=src_ap, scalar=stats[0:np_, rs_c:rs_c + 1],
                                                     in1=gain[0:np_, :], op0=ALU.mult, op1=ALU.mult),
             r=list(r_src) + [("st", rs_c), gain_key], w=[xn_key])

    def transpose_tile(xn_t, np_, xn_key, dst_fn, dst_key, extra=()):
        b = bank()
        tp = banks[b][:, :].bitcast(BF16).rearrange("p (k t) -> p k t", k=KD)
        for k in range(KD):
            P.op("pe", lambda e, k=k: e.transpose(tp[:, k, 0:np_], xn_t[0:np_, k * 128:(k + 1) * 128],
                                                  ident_b[0:np_, 0:np_]),
                 r=[xn_key, "ident_b"], w=[("bank", b)])
        P.op("dve", lambda e: e.tensor_copy(dst_fn(), tp[:, :, 0:np_]), r=[("bank", b)], w=[dst_key], extra=extra)

    norm_tile(meta_sb[:, :], NMETA, X_SS + 16, X_STD + 16, X_RS + 16, ["meta_sb"], xn_meta, "xn_meta", gain_a, "ga")
    transpose_tile(xn_meta, NMETA, "xn_meta", lambda: xnT[:, :, 0:NMETA], ("xnT", "m"))
    def stage_x_norm(i):
        s = i % 4
        q = i // 2
        sx = q % NXS
        norm_tile(ystage[:, sx, i % 2, :], 128, X_SS + i, X_STD + i, X_RS + i,
                  [("xtl", i % 2)] if q == NT // 2 - 1 else [("xt", sx)], xn_x[s], ("xn", s), gain_a, "ga",
                  junk=junk_x)
        if i % 2 == 1 and q + NXS < NT // 2:
            load_xt(q + NXS)

    def stage_x_tr(i):
        s = i % 4
        transpose_tile(xn_x[s], 128, ("xn", s),
                       lambda i=i: xnT[:, :, NMETA + i * 128:NMETA + (i + 1) * 128], ("xnT", i // 4))
    def load_gain_pre_ffn():
        P.dma("sp", lambda e: e.dma_start(out=gain_a[:, :], in_=gains_d[2:3, :].partition_broadcast(128)), "ga", w=["ga"])
        P.dma("sp", lambda e: e.dma_start(out=gain_b[:, :], in_=gains_d[1:2, :].partition_broadcast(128)), "gb", w=["gb"])
    def load_w_out():
        for hh in range(4):
            P.dma("pool", lambda e, hh=hh: e.dma_start(out=w_out[:, hh * 2:(hh + 1) * 2, :],
                                                      in_=w_out_v[:, hh * 2:(hh + 1) * 2, :]), "wo", w=["wo"],
                  extra=fence_x)

    chunk_i = [0]

    def next_chunk():
        n = chunk_i[0]
        chunk_i[0] += 1
        prefetch(n + 4)
        return n % NWR

    def proj(slot, act_T, act_key, col0, ncol, b=None):
        if b is None:
            b = bank()
        fns = [(lambda e, k=k: e.matmul(banks[b][:, 0:ncol], wring[slot][:, k, :], act_T[:, k, col0:col0 + ncol],
                                        start=(k == 0), stop=(k == KD - 1))) for k in range(KD)]
        P.group("pe", fns, r=[("wr", slot), act_key], w=[("bank", b)])
        return b

    ca_bf = [sb(m1, "ca_bf%d" % i, [128, 514], BF16, R) for i in range(2)]
    diag3 = [sb(m1, "diag3_%d" % i, [128, 3, 128], BF16, R) for i in range(2)]

    a_iter = [0]
    ln_enabled = [True]

    def path_A(cc):
        sc, sh, sbb = next_chunk(), next_chunk(), next_chunk()
        d3 = cc % 2
        P.op("dve", lambda e: e.tensor_tensor(
            out=diag3[d3][:, :, :], in0=ident_b[:, :].unsqueeze(1).broadcast_to([128, 3, 128]),
            in1=parT[:, cc, 0:3].unsqueeze(2).broadcast_to([128, 3, 128]), op=ALU.mult),
            r=["ident_b", "parT"], w=[("diag3", d3)])
        bc = proj(sc, xnT, ("xnT", "m"), 0, NMETA)
        bh = proj(sh, xnT, ("xnT", "m"), 0, NMETA)
        a = nextA()
        P.op("act", lambda e, bc=bc, a=a: e.activation(out=tmpA[a][:, 0:NMETA], in_=banks[bc][:, 0:NMETA], func=AF.Copy),
             r=[("bank", bc)], w=[("tmpA", a)])
        P.op("dve", lambda e, bh=bh, a=a: e.tensor_tensor(out=ca_meta[:, cc, :], in0=banks[bh][:, 0:NMETA],
                                                       in1=tmpA[a][:, 0:NMETA], op=ALU.mult),
             r=[("bank", bh), ("tmpA", a)], w=[("ca_meta", cc)])
        def ch(tg):
            col0 = NMETA + tg * 512
            bc = proj(sc, xnT, ("xnT", tg), col0, 512)
            bh = proj(sh, xnT, ("xnT", tg), col0, 512)
            a = nextA()
            cs = (cc * NG + tg) % 2
            P.op("act", lambda e, bc=bc, a=a: e.activation(out=tmpA[a][:, :], in_=banks[bc][:, :], func=AF.Copy),
                 r=[("bank", bc)], w=[("tmpA", a)])
            P.op("dve", lambda e, bh=bh, a=a, cs=cs: e.tensor_tensor(out=ca_bf[cs][:, 2:514], in0=banks[bh][:, :],
                                                                  in1=tmpA[a][:, :], op=ALU.mult),
                 r=[("bank", bh), ("tmpA", a)], w=[("ca", cs)])
            if tg == 0:
                P.op("dve", lambda e, cs=cs: e.tensor_copy(ca_bf[cs][:, 0:2], ca_meta[:, cc, NMETA - 2:NMETA]),
                     r=[("ca_meta", cc)], w=[("cah", cs)])
            else:
                P.op("dve", lambda e, cs=cs: e.tensor_copy(ca_bf[cs][:, 0:2], ca_bf[1 - cs][:, 512:514]),
                     r=[("ca", 1 - cs)], w=[("cah", cs)])

        ch(0)
        for tg in range(NG):
            col0 = NMETA + tg * 512
            cs = (cc * NG + tg) % 2
            if tg + 1 < NG:
                ch(tg + 1)
            bb = proj(sbb, xnT, ("xnT", tg), col0, 512)
            b3 = bank()
            fns = [(lambda e, k=k, b3=b3, cs=cs: e.matmul(banks[b3][:, :], diag3[d3][:, k, :], ca_bf[cs][:, k:k + 512],
                                                          start=(k == 0), stop=(k == 2))) for k in range(3)]
            P.group("pe", fns, r=[("diag3", d3), ("ca", cs), ("cah", cs)], w=[("bank", b3)])
            tb = nextB()
            P.op("act", lambda e, b3=b3, tb=tb: e.activation(out=tmpB[tb][:, :], in_=banks[b3][:, :], func=AF.Copy),
                 r=[("bank", b3)], w=[("tmpB", tb)])
            P.op("dve", lambda e, bb=bb, tb=tb, tg=tg: e.tensor_tensor(
                out=y[:, cc, tg * 512:(tg + 1) * 512], in0=banks[bb][:, :], in1=tmpB[tb][:, :], op=ALU.mult),
                r=[("bank", bb), ("tmpB", tb)], w=[("y", tg)], extra=fence_x)
            if ln_enabled[0]:
                a_iter[0] += 1
                ln_advance(NG + (a_iter[0] * 16 + 11) // 12)

    acc = [sb(m1, "acc_%d" % i, [128, 512], F32, R) for i in range(2)]

    P.dma("sp", lambda e: e.dma_start(out=wrep_sb[:, :, :].rearrange("p c k -> p (c k)"), in_=wrep_d), "setup_wrep",
          w=["wrep"])
    P.op("dve", lambda e: e.tensor_tensor(out=Jt[:, 0:32], in0=ident_f[:, 0:32], in1=ident_f[:, 32:64], op=ALU.add),
         r=["ident_f"], w=["Jt0"])
    P.op("dve", lambda e: e.tensor_tensor(out=Jt[:, 32:64], in0=ident_f[:, 64:96], in1=ident_f[:, 96:128], op=ALU.add),
         r=["ident_f"], w=["Jt1"])
    P.op("dve", lambda e: e.tensor_tensor(out=Jb[:, :], in0=Jt[:, 0:32], in1=Jt[:, 32:64], op=ALU.add),
         r=["Jt0", "Jt1"], w=["Jb"])

    glu_slots = {}

    def B_glu_slots(cc):
        if cc not in glu_slots:
            glu_slots[cc] = (next_chunk(), next_chunk())
        return glu_slots[cc]

    def B_glu_proj(cc, tg, bv=None, bg=None):
        sv, sg_ = B_glu_slots(cc)
        col0 = NMETA + tg * 512
        bv = proj(sv, xnT, ("xnT", tg), col0, 512, b=bv)
        bg = proj(sg_, xnT, ("xnT", tg), col0, 512, b=bg)
        return bv, bg

    def B_glu_epi(cc, tg, bv, bg):
        gs = cc % 2
        a = nextA()
        P.op("act", lambda e, bg=bg, a=a: e.activation(out=tmpA[a][:, :], in_=banks[bg][:, :], func=AF.Sigmoid),
             r=[("bank", bg)], w=[("tmpA", a)])
        P.op("dve", lambda e, bv=bv, a=a, tg=tg: e.tensor_tensor(
            out=g_bf[gs][:, GOFF + tg * 512:GOFF + (tg + 1) * 512], in0=banks[bv][:, :], in1=tmpA[a][:, :], op=ALU.mult),
            r=[("bank", bv), ("tmpA", a)], w=[("g", gs, tg)])

    def B_glu(cc, parked=()):
        sv, sg_ = B_glu_slots(cc)
        gs = cc % 2
        bv = proj(sv, xnT, ("xnT", "m"), 0, NMETA)
        bg = proj(sg_, xnT, ("xnT", "m"), 0, NMETA)
        a = nextA()
        P.op("act", lambda e, bg=bg, a=a: e.activation(out=tmpA[a][:, 0:NMETA], in_=banks[bg][:, 0:NMETA], func=AF.Sigmoid),
             r=[("bank", bg)], w=[("tmpA", a)])
        P.op("dve", lambda e, bv=bv, a=a: e.tensor_tensor(out=g_bf[gs][:, 30:GOFF], in0=banks[bv][:, 0:NMETA],
                                                       in1=tmpA[a][:, 0:NMETA], op=ALU.mult),
             r=[("bank", bv), ("tmpA", a)], w=[("g", gs, "m")])
        for tg in range(NG):
            if tg < len(parked):
                bv, bg = parked[tg]
            else:
                bv, bg = B_glu_proj(cc, tg)
            B_glu_epi(cc, tg, bv, bg)

    def B_rep(cc):
        gs = cc % 2
        gb = grep_bufs[cc % 2]
        gkeys = [("g", gs, t) for t in range(NG)] + [("g", gs, "m"), ("g", gs, "z"), ("g", gs, "z2")]
        for b in range(4):
            for r in range(4):
                P.dma("sp",
                      lambda e, b=b, r=r: e.dma_start(out=gb[32 * r:32 * r + 32, b, 0:WG],
                                                           in_=g_bf[gs][32 * b:32 * b + 32, 16 + r:16 + r + WG]),
                      ("rep", cc % 2, b, r), r=gkeys, w=[("grep", cc % 2, b, r)],
                      extra=fence_x)

    def B_wall(cc):
        wl = cc % 2
        P.op("dve", lambda e: e.tensor_tensor(
            out=Wall[wl][:, :, :], in0=Jb[:, :].unsqueeze(1).broadcast_to([128, 32, 32]),
            in1=wrep_sb[:, cc, :].unsqueeze(2).broadcast_to([128, 32, 32]), op=ALU.mult),
            r=["Jb", "wrep"], w=[("Wall", wl)])

    def B_conv(cc):
        wl = cc % 2
        gb = grep_bufs[cc % 2]
        for tg in range(NG):
            bz = bank()
            fns = [(lambda e, q=q, b=b, bz=bz, tg=tg: e.matmul(
                banks[bz][32 * b:32 * b + 32, :], Wall[wl][:, q * 4 + b, :],
                gb[:, b, tg * 512 + 4 * q:tg * 512 + 4 * q + 512],
                start=(q == 0), stop=(q == 7), tile_position=(0, 32 * b))) for q in range(8) for b in range(4)]
            P.group("pe", fns, r=[("Wall", wl)] + [("grep", cc % 2, b, r) for b in range(4) for r in range(4)],
                    w=[("bank", bz)])
            P.op("act", lambda e, bz=bz, tg=tg: e.activation(
                out=zb[:, cc, tg * 512:(tg + 1) * 512].bitcast(F32R), in_=banks[bz][:, :], func=AF.Identity,
                bias=parT[:, cc, 34:35]),
                r=[("bank", bz), "parT"], w=[("zb", cc, tg)], extra=fence_x)

    _dv = grep[:, :, :].rearrange("p b j -> p (b j)").bitcast(F32)
    _gv = [g[:, :].bitcast(F32) for g in g_bf]
    mean_v = [_dv[:, t * 512:(t + 1) * 512] for t in range(NG)]
    rstd_v = [_gv[t // 2][:, (t % 2) * 512:(t % 2 + 1) * 512] for t in range(NG)]
    fence_B = []
    ln_parts = []

    def ln_stats(tg):
        cols = slice(tg * 512, (tg + 1) * 512)
        b1 = bank()
        b2 = bank()
        fns = [(lambda e, cc=cc: e.matmul(banks[b1][:, :], ones_f[:, :].bitcast(F32R), zb[:, cc, cols].bitcast(F32R),
                                          start=(cc == 0), stop=(cc == 3))) for cc in range(4)]

        def sq(cc):
            a = cc % 2
            P.op("act", lambda e: e.activation(out=zsq[a][:, :].bitcast(F32R), in_=zb[:, cc, cols], func=AF.Square),
                 r=[("zb", cc, tg)], w=[("zsq", a)])

        sq(0)
        sq(1)
        P.group("pe", fns, r=["ones_f"] + [("zb", cc, tg) for cc in range(4)], w=[("bank", b1)])
        for cc in range(4):
            a = cc % 2
            P.op("pe", lambda e, a=a, cc=cc: e.matmul(banks[b2][:, :], ones_f[:, :].bitcast(F32R), zsq[a][:, :].bitcast(F32R),
                                                      start=(cc == 0), stop=(cc == 3)),
                 r=["ones_f", ("zsq", a)], w=[("bank", b2)])
            if cc + 2 < 4:
                sq(cc + 2)
        P.op("act", lambda e: e.activation(out=mean_v[tg], in_=banks[b1][:, :], func=AF.Copy),
             r=[("bank", b1)], w=[("mean", tg)], extra=fence_B)
        P.op("act", lambda e: e.activation(out=mv[:, :], in_=banks[b1][:, :], func=AF.Square),
             r=[("bank", b1)], w=["mv"])
        P.op("dve", lambda e: e.tensor_tensor(out=mv[:, :], in0=banks[b2][:, :], in1=mv[:, :], op=ALU.subtract),
             r=[("bank", b2), "mv"], w=["mv"])
        P.op("act", lambda e: e.activation(out=rstd_v[tg], in_=mv[:, :], func=AF.Sqrt, scale=1.0, bias=EPS_LN),
             r=["mv"], w=[("rstd", tg)], extra=fence_B)
        P.op("dve", lambda e: e.reciprocal(rstd_v[tg], rstd_v[tg]), r=[("rstd", tg)], w=[("rstd", tg)])

    def ln_norm(tg, cc):
        cols = slice(tg * 512, (tg + 1) * 512)
        tb = (tg * 4 + cc) % 2
        P.op("dve", lambda e: e.tensor_tensor(out=acc[tb][:, :], in0=zb[:, cc, cols], in1=mean_v[tg], op=ALU.subtract),
             r=[("zb", cc, tg), ("mean", tg)], w=[("acc", tb)])
        P.op("dve", lambda e: e.tensor_tensor(out=acc[tb][:, :], in0=acc[tb][:, :], in1=rstd_v[tg], op=ALU.mult),
             r=[("acc", tb), ("rstd", tg)], w=[("acc", tb)])
        P.op("act", lambda e: e.activation(out=y[:, 4 + cc, cols], in_=acc[tb][:, :], func=AF.Silu,
                                           scale=parT[:, cc, 35:36], bias=parT[:, cc, 36:37]),
             r=[("acc", tb), "parT"], w=[("y", tg)], extra=fence_x)

    for tg_ in range(NG):
        ln_parts.append(lambda tg_=tg_: ln_stats(tg_))
    for tg_ in range(NG):
        for cc_ in range(4):
            ln_parts.append(lambda tg_=tg_, cc_=cc_: ln_norm(tg_, cc_))
    ln_done = [0]

    def ln_advance(upto):
        while ln_done[0] < min(upto, len(ln_parts)):
            ln_parts[ln_done[0]]()
            ln_done[0] += 1

    fence_x = []

    parked0 = []

    def stage_x_all():
        for i in range(3):
            stage_x_norm(i)
        for i in range(NT):
            if i == 9:
                bank_pool[0] = [0, 1]
                nbank[0] = 0
            stage_x_tr(i)
            if i + 3 < NT:
                stage_x_norm(i + 3)
            if i + 3 == NT - 1:
                P.op("act", lambda e: e.activation(out=stats[:, 268:269], in_=stats[:, X_RS:X_RS + 1], func=AF.Sigmoid),
                     r=[("st", X_RS)], w=[("st", 268)])
            if i in (10, 12, 14):
                t = (i - 10) // 2
                parked0.append(B_glu_proj(0, t, bv=2 + 2 * t, bg=3 + 2 * t))

    for cc in range(4):
        if cc == 0:
            stage_x_all()
            load_gain_pre_ffn()
            fence_x.extend(P.fence())
        if cc == 0:
            B_wall(0)
            B_glu(0, parked=parked0)
            bank_pool[0] = list(range(8))
            B_rep(0)
        if cc + 1 < 4:
            B_wall(cc + 1)
            B_glu(cc + 1)
            B_rep(cc + 1)
        if cc == 3:
            ln_enabled[0] = False
            path_A(0)
            ln_enabled[0] = True
        B_conv(cc)
        if cc == 2:
            fence_x.extend(P.fence())
        if cc == 1:
            load_w_out()
    fence_B.extend(P.fence())
    ln_advance(NG)
    for cc in range(1, 4):
        path_A(cc)
    ln_advance(len(ln_parts))

    if debug:
        P.dma("sp", lambda e: e.dma_start(out=d_xnT, in_=xnT[:, :, :].rearrange("p k t -> p (k t)")), "dbg",
              r=[("xnT", "m")] + [("xnT", t) for t in range(NG)])
        P.dma("sp", lambda e: e.dma_start(out=d_zb, in_=zb[:, :, :].rearrange("p k t -> p (k t)")), "dbg",
              r=[("zb", c, t) for c in range(4) for t in range(NG)])
        P.dma("sp", lambda e: e.dma_start(out=d_y, in_=y[:, :, :].rearrange("p k t -> p (k t)")), "dbg",
              r=[("y", t) for t in range(NG)])
        P.dma("sp", lambda e: e.dma_start(out=d_parT, in_=parT[:, :, :].rearrange("p k t -> p (k t)")), "dbg", r=["parT"])
    fence1 = P.fence()
    m1.close()
    hstack = contextlib.ExitStack()
    h = sb(hstack, "h", [128, NT, D], F32, R)
    t1_tok = sb(hstack, "t1_tok", [128, D], F32, R)

    def post_norm_residual(i, bks, SH, ST, STD, RS, gain, gain_key, first=False, extra=()):
        if first:
            dst = lambda dh: h[:, i, dh * 512:(dh + 1) * 512]
            dkey = lambda dh: ("h", i)
        else:
            dst = lambda dh: t1_tok[:, dh * 512:(dh + 1) * 512]
            dkey = lambda dh: ("t1", dh)
        for dh in range(2):
            P.op("act", lambda e, dh=dh: e.activation(out=dst(dh), in_=banks[bks[dh]][:, :],
                                                     func=AF.Square, accum_out=scol(SH + 2 * i + dh)),
                 r=[("bank", bks[dh])], w=[dkey(dh), ("st", SH + 2 * i + dh)], extra=extra)
        P.op("dve", lambda e: e.tensor_tensor(out=scol(ST + i), in0=scol(SH + 2 * i), in1=scol(SH + 2 * i + 1), op=ALU.add),
             r=[("st", SH + 2 * i), ("st", SH + 2 * i + 1)], w=[("st", ST + i)])
        P.op("act", lambda e: e.activation(out=scol(STD + i), in_=scol(ST + i), func=AF.Sqrt, scale=1.0 / D, bias=EPS_RMS),
             r=[("st", ST + i)], w=[("st", STD + i)])
        P.op("dve", lambda e: e.reciprocal(scol(RS + i), scol(STD + i)), r=[("st", STD + i)], w=[("st", RS + i)])
        for dh in range(2):
            P.op("dve", lambda e, dh=dh: e.scalar_tensor_tensor(
                out=dst(dh), in0=banks[bks[dh]][:, :], scalar=scol(RS + i),
                in1=gain[:, dh * 512:(dh + 1) * 512], op0=ALU.mult, op1=ALU.mult),
                r=[("bank", bks[dh]), ("st", RS + i), gain_key], w=[dkey(dh)])
        if first:
            P.dma("pool", lambda e: e.dma_start(out=h[:, i, :], in_=x_d[i * 128:(i + 1) * 128, :], accum_op=ALU.add),
                  ("hacc", i), r=[("h", i)], w=[("h", i)])
        else:
            P.op("dve", lambda e: e.tensor_tensor(out=h[:, i, :], in0=h[:, i, :], in1=t1_tok[:, :], op=ALU.add),
                 r=[("t1", 0), ("t1", 1), ("h", i)], w=[("h", i)])

    def xn2_norm(hf, j):
        i = hf * 8 + j
        s = i % 2
        norm_tile(h[:, i, :], 128, F_SS + i, F_STD + i, F_RS + i, [("h", i)], xn[s], ("xn", s), gain_a, "ga")

    def xn2_tr(hf, j):
        i = hf * 8 + j
        s = i % 2
        transpose_tile(xn[s], 128, ("xn", s), lambda j=j: xn2T[:, :, j * 128:(j + 1) * 128], ("xn2T", j // 4))

    prefetch(chunk_i[0] + NWR)
    last_out_mm = [None]
    for i in range(NT):
        bks = []
        for dh in range(2):
            b = bank()
            fns = [(lambda e, k=k, b=b, dh=dh, i=i: e.matmul(banks[b][:, :], y[:, k, i * 128:(i + 1) * 128],
                                                        w_out[:, k, dh * 512:(dh + 1) * 512],
                                                        start=(k == 0), stop=(k == KD - 1))) for k in range(KD)]
            last_out_mm[0] = P.group("pe", fns, r=[("y", i // 4), "wo"], w=[("bank", b)])
            bks.append(b)
        post_norm_residual(i, bks, M_SH, M_ST, M_STD, M_RS, gain_b, "gb", first=True, extra=fence1)
        j = i - 4
        if 0 <= j < 8:
            xn2_norm(0, j)
            if j >= 1:
                xn2_tr(0, j - 1)
        if j == 8:
            xn2_tr(0, 7)
    P.dma("sp", lambda e: e.dma_start(out=gain_b[:, :], in_=gains_d[3:4, :].partition_broadcast(128)), "gb", w=["gb"])

    if debug:
        P.dma("sp", lambda e: e.dma_start(out=d_h1, in_=h[:, :, :].rearrange("p k t -> p (k t)")), "dbg",
              r=[("h", i) for i in range(NT)])
    fence2 = P.fence()
    fence_y = [last_out_mm[0]] + ([("D", "dbg", P.dcnt["dbg"])] if debug else [])
    mixl.close()
    ffn = contextlib.ExitStack()
    actb = sb(ffn, "actb", [128, NF, 1024], BF16)
    w_down = sb(ffn, "w_down", [128, NF, D], BF16)
    sring = [sb(ffn, "sring%d" % i, [128, 512], F32) for i in range(2)]
    def load_wd(f):
        P.dma("pool", lambda e, f=f: e.dma_start(out=w_down[:, f:f + 1, :], in_=w_down_v[:, f:f + 1, :]),
              "wd", w=["wd"], extra=fence1 + fence_y)

    def down_mm(hf, j):
        bks = []
        for dh in range(2):
            b = bank()
            fns = [(lambda e, f=f, b=b, dh=dh: e.matmul(banks[b][:, :], actb[:, f, j * 128:(j + 1) * 128],
                                                        w_down[:, f, dh * 512:(dh + 1) * 512],
                                                        start=(f == 0), stop=(f == NF - 1))) for f in range(NF)]
            P.group("pe", fns, r=[("act", j // 4), "wd"], w=[("bank", b)])
            bks.append(b)
        return bks

    def down_epi(hf, j, bks):
        i = hf * 8 + j
        post_norm_residual(i, bks, G_SH, G_ST, G_STD, G_RS, gain_b, "gb")
        P.dma("sp", lambda e: e.dma_start(out=out_d[i * 128:(i + 1) * 128, :], in_=h[:, i, :]), "out", r=[("h", i)])

    for hf in range(2):
        for f in range(NF):
            sg_, su_ = next_chunk(), next_chunk()
            if hf == 0:
                load_wd(f)
            for t2 in range(2):
                ba = proj(sg_, xn2T, ("xn2T", t2), t2 * 512, 512)
                bu = proj(su_, xn2T, ("xn2T", t2), t2 * 512, 512)
                sr = (f * 2 + t2) % 2
                P.op("act", lambda e, ba=ba, sr=sr: e.activation(out=sring[sr][:, :], in_=banks[ba][:, :], func=AF.Silu),
                     r=[("bank", ba)], w=[("sring", sr)], extra=fence1)
                P.op("dve", lambda e, bu=bu, sr=sr, f=f, t2=t2: e.tensor_tensor(
                    out=actb[:, f, t2 * 512:(t2 + 1) * 512], in0=banks[bu][:, :], in1=sring[sr][:, :], op=ALU.mult),
                    r=[("bank", bu), ("sring", sr)], w=[("act", t2)], extra=fence_y)
        if hf == 0:
            xn2_norm(1, 0)
        for j in range(8):
            bks = down_mm(hf, j)
            if hf == 0:
                xn2_tr(1, j)
                if j + 1 < 8:
                    xn2_norm(1, j + 1)
            down_epi(hf, j, bks)
    P.wait("sp", [("D", "out", P.dcnt["out"])] + ([("D", "dbg", P.dcnt["dbg"])] if debug else []))

    P.emit(nc, root)
    ffn.close()
    hstack.close()
    root.close()
    return nc


_NC_CACHE = {}


def kernel(**inputs):
    f32 = lambda a: np.ascontiguousarray(np.asarray(a), dtype=np.float32)
    x = f32(inputs["x"])
    B = x.shape[0]
    meta = f32(inputs["meta_tokens"])
    gains = np.concatenate([f32(inputs[k]).reshape(1, D) for k in
                            ("pre_mix_norm", "post_mix_norm", "pre_ffn_norm", "post_ffn_norm")], axis=0)
    par = np.concatenate([f32(inputs["conv_a_w"]).reshape(3, WA), f32(inputs["conv_b_w"]).reshape(31, WA),
                          f32(inputs["conv_b_bias"]).reshape(1, WA), f32(inputs["ln_b_gain"]).reshape(1, WA),
                          f32(inputs["ln_b_bias"]).reshape(1, WA)], axis=0)
    wb32 = np.concatenate([f32(inputs["conv_b_w"]).reshape(31, WA), np.zeros((1, WA), np.float32)], axis=0)
    wrep = np.ascontiguousarray(wb32.reshape(8, 4, 4, 4, 32).transpose(1, 4, 2, 0, 3)).reshape(128, 128)
    shared = dict(meta=meta, gains=np.ascontiguousarray(gains), par=np.ascontiguousarray(par), wrep=wrep,
                  w_in=f32(inputs["w_in"]).reshape(D, EIN), w_out=f32(inputs["w_out"]).reshape(D, D),
                  w_gate=f32(inputs["w_gate"]).reshape(D, DFF), w_up=f32(inputs["w_up"]).reshape(D, DFF),
                  w_down=f32(inputs["w_down"]).reshape(DFF, D))
    if "nc" not in _NC_CACHE:
        _NC_CACHE["nc"] = build_nc()
    nc = _NC_CACHE["nc"]
    in_maps = [dict(shared, x=np.ascontiguousarray(x[b])) for b in range(B)]
    res = run_bass_kernel_spmd(nc, in_maps, core_ids=list(range(B)))
    return np.stack([np.asarray(r["out"], dtype=np.float32) for r in res.results], axis=0)
```
